# Optimizing a Trainium2 kernel written in Bass

```python
import jax, jax.numpy as jnp
from jax import lax
import numpy as np

D_MODEL = 2048
BATCH = 2
SEQ = 4096
DEPTH = 4

N_MIXERS = 2
N_RGLRU = (DEPTH + 1) // 2
N_RWKV = DEPTH // 2
D_FF = 4 * D_MODEL
D_RNN = D_MODEL
RG_HEADS = 8
RG_BLOCK = D_RNN // RG_HEADS
CONV_W = 4
RG_C = 8.0
RW_HEAD = 64
RW_HEADS = D_MODEL // RW_HEAD
DECAY_LORA = 96
AAA_LORA = 96
GATE_LORA = 256
RMS_EPS = 1e-6
GN_EPS = 64e-5

kernel_name = 'hybrid_rglru_rwkv7_sqrelu_trunk'


def rmsnorm(x, g):
    xf = x.astype(jnp.float32)
    y = xf * lax.rsqrt(jnp.mean(xf * xf, axis=-1, keepdims=True) + RMS_EPS)
    return (y * g.astype(jnp.float32)).astype(x.dtype)


def causal_depthwise_conv(x, w, b):
    s = x.shape[1]
    xp = jnp.pad(x, ((0, 0), (CONV_W - 1, 0), (0, 0)))
    out = b
    for k in range(CONV_W):
        out = out + w[k] * xp[:, k:k + s]
    return out


def block_diag_linear(x, w, b):
    xb = x.reshape(x.shape[:-1] + (RG_HEADS, RG_BLOCK))
    return jnp.einsum('bshi,hij->bshj', xb, w).reshape(x.shape) + b


def rglru_block(x, w_in, conv_w, conv_b, gx_w, gx_b, ga_w, ga_b, lam, w_out):
    proj = x @ w_in
    gate_branch, rec = jnp.split(proj, 2, axis=-1)
    rec = causal_depthwise_conv(rec, conv_w, conv_b)
    recf = rec.astype(jnp.float32)
    i_t = jax.nn.sigmoid(block_diag_linear(recf, gx_w, gx_b).astype(jnp.float32))
    r_t = jax.nn.sigmoid(block_diag_linear(recf, ga_w, ga_b).astype(jnp.float32))
    log_a = RG_C * r_t * jax.nn.log_sigmoid(lam.astype(jnp.float32))
    a_t = jnp.exp(log_a)
    mult = jnp.sqrt(jnp.maximum(-jnp.expm1(2.0 * log_a), 0.0))
    b_t = mult * (i_t * recf)

    def combine(e1, e2):
        a1, b1 = e1
        a2, b2 = e2
        return a1 * a2, a2 * b1 + b2

    _, h = lax.associative_scan(combine, (a_t, b_t), axis=1)
    y = h.astype(x.dtype) * jax.nn.gelu(gate_branch)
    return y @ w_out


def rwkv7_time_mix(x, mu, w_rkv, w0, w1, w2, a0, a1, a2, g1, g2,
                   k_k, k_a, r_k, ln_g, ln_b, w_out):
    bsz, s, d = x.shape
    f32 = jnp.float32
    xx = jnp.pad(x, ((0, 0), (1, 0), (0, 0)))[:, :-1] - x
    xm = x[:, :, None, :] + xx[:, :, None, :] * mu
    rkv = jnp.einsum('bsjd,jde->bsje', xm[:, :, :3], w_rkv)
    r, k, v = rkv[:, :, 0], rkv[:, :, 1], rkv[:, :, 2]
    xw, xa, xg = xm[:, :, 3], xm[:, :, 4], xm[:, :, 5]
    w = -jax.nn.softplus(-(w0 + jnp.tanh(xw @ w1) @ w2)) - 0.5
    decay = jnp.exp(-jnp.exp(w.astype(f32)))
    a = jax.nn.sigmoid((a0 + (xa @ a1) @ a2).astype(f32))
    g = jax.nn.sigmoid(xg @ g1) @ g2

    def heads(t):
        return t.astype(f32).reshape(bsz, s, RW_HEADS, RW_HEAD)

    kk = heads(k * k_k)
    kk = kk / jnp.maximum(jnp.sqrt(jnp.sum(kk * kk, axis=-1, keepdims=True)), 1e-12)
    k_mod = k.astype(f32) * (1.0 + (a - 1.0) * k_a.astype(f32))
    r_h, k_h, v_h, w_h, a_h = heads(r), heads(k_mod), heads(v), heads(decay), heads(a)

    def step(state, inp):
        r_t, w_t, k_t, v_t, kk_t, a_t = inp
        sa = jnp.einsum('bhij,bhj->bhi', state, kk_t)
        state = (state * w_t[:, :, None, :]
                 - sa[..., None] * (kk_t * a_t)[:, :, None, :]
                 + v_t[..., None] * k_t[:, :, None, :])
        y_t = jnp.einsum('bhij,bhj->bhi', state, r_t)
        return state, y_t

    xs = tuple(jnp.moveaxis(t, 1, 0) for t in (r_h, w_h, k_h, v_h, kk, a_h))
    state0 = jnp.zeros((bsz, RW_HEADS, RW_HEAD, RW_HEAD), f32)
    _, y = lax.scan(step, state0, xs)
    y = jnp.moveaxis(y, 0, 1)
    mean = jnp.mean(y, axis=-1, keepdims=True)
    var = jnp.mean(jnp.square(y - mean), axis=-1, keepdims=True)
    y = (y - mean) * lax.rsqrt(var + GN_EPS)
    y = y.reshape(bsz, s, d) * ln_g.astype(f32) + ln_b.astype(f32)
    bonus = jnp.sum(r_h * k_h * r_k.astype(f32), axis=-1, keepdims=True) * v_h
    y = y + bonus.reshape(bsz, s, d)
    return (y * g.astype(f32)).astype(x.dtype) @ w_out


def sq_relu_mlp(x, w_up, w_down):
    return jnp.square(jax.nn.relu(x @ w_up)) @ w_down


def setup_inputs(seed: int = 0) -> dict:
    key = jax.random.key(seed)
    ks = jax.random.split(key, 40)
    f32 = jnp.float32

    def nrm(k, shape, scale):
        return jax.random.normal(k, shape, f32) * scale

    x = jax.random.normal(ks[0], (BATCH, SEQ, D_MODEL), f32)
    norm_mix_g = 1.0 + nrm(ks[1], (DEPTH, D_MODEL), 0.02)
    norm_mlp_g = 1.0 + nrm(ks[2], (DEPTH, D_MODEL), 0.02)
    w_mlp_up = nrm(ks[3], (DEPTH, D_MODEL, D_FF), D_MODEL ** -0.5)
    w_mlp_down = nrm(ks[4], (DEPTH, D_FF, D_MODEL), 0.5 * D_FF ** -0.5)
    final_norm_g = 1.0 + nrm(ks[5], (D_MODEL,), 0.02)
    rg_w_in = nrm(ks[6], (N_RGLRU, D_MODEL, 2 * D_RNN), D_MODEL ** -0.5)
    rg_conv_w = nrm(ks[7], (N_RGLRU, CONV_W, D_RNN), CONV_W ** -0.5)
    rg_conv_b = nrm(ks[8], (N_RGLRU, D_RNN), 0.02)
    rg_gx_w = nrm(ks[9], (N_RGLRU, RG_HEADS, RG_BLOCK, RG_BLOCK), RG_BLOCK ** -0.5)
    rg_gx_b = nrm(ks[10], (N_RGLRU, D_RNN), 0.1)
    rg_ga_w = nrm(ks[11], (N_RGLRU, RG_HEADS, RG_BLOCK, RG_BLOCK), RG_BLOCK ** -0.5)
    rg_ga_b = nrm(ks[12], (N_RGLRU, D_RNN), 0.1)
    u = jax.random.uniform(ks[13], (N_RGLRU, D_RNN), f32, minval=0.9, maxval=0.999)
    sgm = u ** (1.0 / RG_C)
    rg_lambda = jnp.log(sgm) - jnp.log1p(-sgm)
    rg_w_out = nrm(ks[14], (N_RGLRU, D_RNN, D_MODEL), 0.5 * D_RNN ** -0.5)
    rw_mu = jax.random.uniform(ks[15], (N_RWKV, 6, D_MODEL), f32)
    rw_w_rkv = nrm(ks[16], (N_RWKV, 3, D_MODEL, D_MODEL), D_MODEL ** -0.5)
    ratio = jnp.arange(D_MODEL, dtype=f32) / (D_MODEL - 1)
    rw_w0 = -6.0 + 5.0 * ratio ** 1.35 + nrm(ks[17], (N_RWKV, D_MODEL), 0.1)
    rw_w1 = nrm(ks[18], (N_RWKV, D_MODEL, DECAY_LORA), D_MODEL ** -0.5)
    rw_w2 = nrm(ks[19], (N_RWKV, DECAY_LORA, D_MODEL), 0.3 * DECAY_LORA ** -0.5)
    rw_a0 = nrm(ks[20], (N_RWKV, D_MODEL), 0.1)
    rw_a1 = nrm(ks[21], (N_RWKV, D_MODEL, AAA_LORA), D_MODEL ** -0.5)
    rw_a2 = nrm(ks[22], (N_RWKV, AAA_LORA, D_MODEL), 0.5 * AAA_LORA ** -0.5)
    rw_g1 = nrm(ks[23], (N_RWKV, D_MODEL, GATE_LORA), D_MODEL ** -0.5)
    rw_g2 = nrm(ks[24], (N_RWKV, GATE_LORA, D_MODEL), GATE_LORA ** -0.5)
    rw_k_k = 0.85 + nrm(ks[25], (N_RWKV, D_MODEL), 0.05)
    rw_k_a = 1.0 + nrm(ks[26], (N_RWKV, D_MODEL), 0.05)
    rw_r_k = nrm(ks[27], (N_RWKV, RW_HEADS, RW_HEAD), 0.1)
    rw_ln_g = 1.0 + nrm(ks[28], (N_RWKV, D_MODEL), 0.02)
    rw_ln_b = nrm(ks[29], (N_RWKV, D_MODEL), 0.02)
    rw_w_out = nrm(ks[30], (N_RWKV, D_MODEL, D_MODEL), 0.5 * D_MODEL ** -0.5)
    return {
        'x': x, 'norm_mix_g': norm_mix_g, 'norm_mlp_g': norm_mlp_g,
        'w_mlp_up': w_mlp_up, 'w_mlp_down': w_mlp_down, 'final_norm_g': final_norm_g,
        'rg_w_in': rg_w_in, 'rg_conv_w': rg_conv_w, 'rg_conv_b': rg_conv_b,
        'rg_gx_w': rg_gx_w, 'rg_gx_b': rg_gx_b, 'rg_ga_w': rg_ga_w, 'rg_ga_b': rg_ga_b,
        'rg_lambda': rg_lambda, 'rg_w_out': rg_w_out,
        'rw_mu': rw_mu, 'rw_w_rkv': rw_w_rkv, 'rw_w0': rw_w0, 'rw_w1': rw_w1, 'rw_w2': rw_w2,
        'rw_a0': rw_a0, 'rw_a1': rw_a1, 'rw_a2': rw_a2, 'rw_g1': rw_g1, 'rw_g2': rw_g2,
        'rw_k_k': rw_k_k, 'rw_k_a': rw_k_a, 'rw_r_k': rw_r_k,
        'rw_ln_g': rw_ln_g, 'rw_ln_b': rw_ln_b, 'rw_w_out': rw_w_out,
    }


def reference(x, norm_mix_g, norm_mlp_g, w_mlp_up, w_mlp_down, final_norm_g,
              rg_w_in, rg_conv_w, rg_conv_b, rg_gx_w, rg_gx_b, rg_ga_w, rg_ga_b,
              rg_lambda, rg_w_out,
              rw_mu, rw_w_rkv, rw_w0, rw_w1, rw_w2, rw_a0, rw_a1, rw_a2, rw_g1, rw_g2,
              rw_k_k, rw_k_a, rw_r_k, rw_ln_g, rw_ln_b, rw_w_out):
    h = x
    for i in range(DEPTH):
        hn = rmsnorm(h, norm_mix_g[i])
        j = i // N_MIXERS
        if i % N_MIXERS == 0:
            mix = rglru_block(hn, rg_w_in[j], rg_conv_w[j], rg_conv_b[j],
                              rg_gx_w[j], rg_gx_b[j], rg_ga_w[j], rg_ga_b[j],
                              rg_lambda[j], rg_w_out[j])
        else:
            mix = rwkv7_time_mix(hn, rw_mu[j], rw_w_rkv[j], rw_w0[j], rw_w1[j], rw_w2[j],
                                 rw_a0[j], rw_a1[j], rw_a2[j], rw_g1[j], rw_g2[j],
                                 rw_k_k[j], rw_k_a[j], rw_r_k[j],
                                 rw_ln_g[j], rw_ln_b[j], rw_w_out[j])
        h = h + mix
        hn = rmsnorm(h, norm_mlp_g[i])
        h = h + sq_relu_mlp(hn, w_mlp_up[i], w_mlp_down[i])
    return rmsnorm(h, final_norm_g)
```

```python
import numpy as np
import ml_dtypes
import concourse.bass as bass
import concourse.mybir as mybir
from concourse.bass_utils import run_bass_kernel_spmd

F32 = mybir.dt.float32
BF16 = mybir.dt.bfloat16
AF = mybir.ActivationFunctionType
ALU = mybir.AluOpType

D = 2048
S = 4096
B = 2
DFF = 8192
DEPTH = 4
NCORE = 8
TOK = 1024
CH = 512


class Tk:
    __slots__ = ("lw", "rd")

    def __init__(self):
        self.lw = None
        self.rd = {}


COMPUTE = ("pe", "act", "dve", "pool")
SEM_CHUNK = 16000
NSEM_PER_ENG = 6
DMA_SLOTS = 8


class Rec:
    def __init__(self):
        self.ops = []

    def add(self, eng, fn, reads=(), writes=(), dma=False):
        i = len(self.ops)
        deps = {}
        for t in reads:
            if t.lw is not None:
                deps[t.lw] = True
        for t in writes:
            if t.lw is not None:
                deps.setdefault(t.lw, False)
            for r in t.rd.values():
                if isinstance(r, list):
                    for rr in r:
                        deps.setdefault(rr, False)
                else:
                    deps.setdefault(r, False)
        for t in reads:
            if dma:
                t.rd.setdefault(("d", eng), []).append(i)
            else:
                t.rd[eng] = i
        for t in writes:
            t.lw = i
            t.rd = {}
        self.ops.append((eng, fn, deps, dma))
        return i

    def pe(self, fn, reads=(), writes=()):
        return self.add("pe", fn, reads, writes)

    def act(self, fn, reads=(), writes=()):
        return self.add("act", fn, reads, writes)

    def dve(self, fn, reads=(), writes=()):
        return self.add("dve", fn, reads, writes)

    def pool(self, fn, reads=(), writes=()):
        return self.add("pool", fn, reads, writes)

    def dma(self, eng, fn, reads=(), writes=()):
        return self.add(eng, fn, reads, writes, dma=True)

    @staticmethod
    def _skip(eng, dma, deng, ddma, raw):
        if dma or ddma:
            return False
        if eng != deng:
            return False
        if eng == "pe":
            return True
        return not raw

    def emit(self, nc, sems, dsems):
        ops = self.ops
        n = len(ops)
        needed = [False] * n
        for i in range(n):
            eng, fn, deps, dma = ops[i]
            for d, raw in deps.items():
                deng, _, _, ddma = ops[d]
                if self._skip(eng, dma, deng, ddma, raw):
                    continue
                needed[d] = True
        ticket = [None] * n
        ccount = {e: 0 for e in COMPUTE}
        dcount = {}
        for i in range(n):
            eng, fn, deps, dma = ops[i]
            if dma:
                j = dcount.get(eng, 0)
                dcount[eng] = j + 1
                ticket[i] = ("d", eng, j % DMA_SLOTS, 16 * (j // DMA_SLOTS + 1), j)
            elif needed[i]:
                assert fn is not None
                ccount[eng] += 1
                t = ccount[eng]
                ticket[i] = ("c", eng, (t - 1) // SEM_CHUNK, (t - 1) % SEM_CHUNK + 1, t)
        for e in COMPUTE:
            assert ccount[e] <= SEM_CHUNK * NSEM_PER_ENG, (e, ccount[e])
        per_eng = {}
        for i in range(n):
            per_eng.setdefault(ops[i][0], []).append(i)

        def run_engine(ename, e):
            seen = {}
            for i in per_eng.get(ename, []):
                eng, fn, deps, dma = ops[i]
                waits = {}
                for d, raw in deps.items():
                    deng, _, _, ddma = ops[d]
                    if self._skip(eng, dma, deng, ddma, raw):
                        continue
                    tk = ticket[d]
                    if tk[0] == "c":
                        key = ("c", tk[1])
                        val = tk[4]
                        if seen.get(key, 0) >= val:
                            continue
                        if waits.get(key, (0,))[0] < val:
                            waits[key] = (val, sems[tk[1]][tk[2]], tk[3])
                    else:
                        key = ("d", tk[1], tk[2])
                        val = tk[3]
                        if seen.get(key, 0) >= val:
                            continue
                        if waits.get(key, (0,))[0] < val:
                            waits[key] = (val, dsems[tk[1]][tk[2]], val)
                if dma:
                    tk = ticket[i]
                    if tk[4] >= DMA_SLOTS:
                        key = ("d", tk[1], tk[2])
                        val = tk[3] - 16
                        if seen.get(key, 0) < val and waits.get(key, (0,))[0] < val:
                            waits[key] = (val, dsems[tk[1]][tk[2]], val)
                for key, (val, sem, sval) in waits.items():
                    e.wait_ge(sem, sval)
                    seen[key] = val
                if fn is None:
                    continue
                ins = fn(e)
                tk = ticket[i]
                if tk is not None:
                    if tk[0] == "c":
                        ins.then_inc(sems[tk[1]][tk[2]], 1)
                    else:
                        ins.then_inc(dsems[tk[1]][tk[2]], 16)

        return run_engine


class Ctx:
    def __init__(self, nc, stack):
        self.nc = nc
        self.stack = stack
        self.rec = Rec()
        self.n = 0

    def sb(self, shape, dt, name=None):
        self.n += 1
        return self.stack.enter_context(self.nc.sbuf_tensor(name or f"sb{self.n}", list(shape), dt))

    def ps(self, shape, dt=F32, name=None):
        self.n += 1
        return self.stack.enter_context(self.nc.psum_tensor(name or f"ps{self.n}", list(shape), dt))

    def finish(self, dma_engs=("sp", "pool")):
        nc = self.nc
        sems = {}
        for e in COMPUTE:
            sems[e] = [self.stack.enter_context(nc.semaphore(f"s_{e}{k}")) for k in range(NSEM_PER_ENG)]
        dsems = {}
        for e in dma_engs:
            dsems[e] = [self.stack.enter_context(nc.semaphore(f"d_{e}{k}")) for k in range(DMA_SLOTS)]
        run_engine = self.rec.emit(nc, sems, dsems)
        with nc.Block() as block:
            @block.tensor
            def _(e):
                run_engine("pe", e)

            @block.scalar
            def _(e):
                run_engine("act", e)

            @block.vector
            def _(e):
                run_engine("dve", e)

            @block.gpsimd
            def _(e):
                run_engine("pool", e)

            @block.sync
            def _(e):
                run_engine("sp", e)


class PsumPool:
    def __init__(self, cx, nbanks=8):
        self.tiles = [cx.ps([128, 512], F32) for _ in range(nbanks)]
        self.tk = [Tk() for _ in range(nbanks)]
        self.i = 0

    def get(self):
        k = self.i % len(self.tiles)
        self.i += 1
        return self.tiles[k], self.tk[k]


def rmsnorm_fm(cx, pp, h_tiles, h_tk, g_sb, g_tk, gcol0, ntok, out_tiles, out_tk, ones_bf, ones_tk,
               scratch, scratch_tk, rstd, rstd_tk, after=None):
    rec = cx.rec
    nk = len(h_tiles)
    nblk = ntok // 512
    pss = [pp.get() for _ in range(nblk)]
    for k in range(nk):
        sq, sq_tk = scratch[k % len(scratch)], scratch_tk[k % len(scratch)]
        rec.act(lambda e, o=sq, i=h_tiles[k]: e.activation(o, i, AF.Square), [h_tk[k]], [sq_tk])
        for tb in range(nblk):
            ps, ps_tk = pss[tb]
            rec.pe(lambda e, o=ps[:, :], l=ones_bf, r=sq[:, tb * 512:(tb + 1) * 512], k=k:
                   e.matmul(o, l, r, start=(k == 0), stop=(k == nk - 1)),
                   [sq_tk, ones_tk], [ps_tk])
    for tb in range(nblk):
        ps, ps_tk = pss[tb]
        sl = slice(tb * 512, (tb + 1) * 512)
        rec.act(lambda e, o=rstd[:, sl], i=ps[:, :]: e.activation(o, i, AF.Sqrt, bias=EPS_AP[0], scale=1.0 / D),
                [ps_tk, EPS_TK[0]], [rstd_tk])
    rec.dve(lambda e, o=rstd, i=rstd: e.reciprocal(o, i), [rstd_tk], [rstd_tk])
    for k in range(nk):
        rec.dve(lambda e, o=out_tiles[k], i=h_tiles[k], g=g_sb[:, gcol0 + k:gcol0 + k + 1], r=rstd:
                e.scalar_tensor_tensor(o, i, g, r, ALU.mult, ALU.mult),
                [h_tk[k], g_tk, rstd_tk], [out_tk[k]])
        if after is not None:
            after(k)


EPS_AP = [None]
EPS_TK = [None]


def setup_consts(cx):
    rec = cx.rec
    ones = cx.sb([128, 128], BF16, "ones")
    ones_tk = Tk()
    rec.dve(lambda e: e.memset(ones[:, :], 1.0), [], [ones_tk])
    eps = cx.sb([128, 1], F32, "eps")
    eps_tk = Tk()
    rec.dve(lambda e: e.memset(eps[:, :], 1e-6), [], [eps_tk])
    EPS_AP[0] = eps[:, 0:1]
    EPS_TK[0] = eps_tk
    return ones[:, :], ones_tk


class WStream:
    def __init__(self, cx, nbuf=3, kt=16, ncol=512):
        self.cx = cx
        self.bufs = [cx.sb([128, kt, ncol], BF16) for _ in range(nbuf)]
        self.tks = [Tk() for _ in range(nbuf)]
        self.i = 0
        self.kt = kt

    def load(self, w_ap_rows_cols, dram_tk):
        k = self.i % len(self.bufs)
        self.i += 1
        buf, tk = self.bufs[k], self.tks[k]
        src = w_ap_rows_cols.rearrange("(kc p) n -> p kc n", p=128)
        kt = src.shape[1]
        nco = src.shape[2]
        self.cx.rec.dma("pool", lambda e, o=buf[:, 0:kt, 0:nco], i=src: e.dma_start(out=o, in_=i), [dram_tk], [tk])
        return buf, tk


def build_mlp_kernel(final_norm):
    from contextlib import ExitStack
    nc = bass.Bass("TRN2", target_bir_lowering=False)
    hT_d = nc.dram_tensor("hT", [D, TOK], F32, kind="ExternalInput").ap()
    parts_d = nc.dram_tensor("parts", [4, D, TOK], F32, kind="ExternalInput").ap()
    gm_d = nc.dram_tensor("g_mlp", [128, 16], F32, kind="ExternalInput").ap()
    go_d = nc.dram_tensor("g_out", [128, 16], F32, kind="ExternalInput").ap()
    wu_d = nc.dram_tensor("w_up", [D, DFF], F32, kind="ExternalInput").ap()
    wd_d = nc.dram_tensor("w_down", [DFF, D], F32, kind="ExternalInput").ap()
    h2_d = nc.dram_tensor("h2T", [D, TOK], F32, kind="ExternalOutput").ap()
    odt = F32 if final_norm else BF16
    hn_d = nc.dram_tensor("hnT", [D, TOK], odt, kind="ExternalOutput").ap()

    with ExitStack() as stack:
        cx = Ctx(nc, stack)
        rec = cx.rec
        dr = Tk()
        ones, ones_tk = setup_consts(cx)
        pp = PsumPool(cx, 8)
        KT = D // 128
        h_sb = cx.sb([128, KT, TOK], F32, "h")
        h_tk = [Tk() for _ in range(KT)]
        hn_sb = cx.sb([128, KT, TOK], BF16, "hn")
        hn_tk = [Tk() for _ in range(KT)]
        HT = 8
        hid = cx.sb([128, HT, TOK], BF16, "hid")
        hid_tk = [[Tk() for _ in range(2)] for _ in range(HT)]
        g_sb = cx.sb([128, 32], F32, "g")
        g_tk = Tk()
        tmp = cx.sb([128, 4, 512], F32, "tmp")
        tmp_tk = [Tk() for _ in range(4)]
        tmpi = [0]

        def gettmp():
            i = tmpi[0] % 4
            tmpi[0] += 1
            return tmp[:, i, :], tmp_tk[i]
        scr = cx.sb([128, 2, TOK], BF16, "scr")
        scr_tk = [Tk(), Tk()]
        rstd = cx.sb([128, TOK], F32, "rstd")
        rstd_tk = Tk()

        rec.dma("sp", lambda e: e.dma_start(out=g_sb[:, 0:16], in_=gm_d), [dr], [g_tk])
        rec.dma("sp", lambda e: e.dma_start(out=g_sb[:, 16:32], in_=go_d), [dr], [g_tk])
        hT_v = hT_d.rearrange("(kc p) t -> p kc t", p=128)
        parts_v = parts_d.rearrange("j (kc p) t -> j p kc t", p=128)
        for k in range(KT):
            rec.dma("sp", lambda e, k=k: e.dma_start(out=h_sb[:, k, :], in_=hT_v[:, k, :]), [dr], [h_tk[k]])
            for j in range(4):
                for tb in range(2):
                    sl = slice(tb * 512, (tb + 1) * 512)
                    t_ap, t_tk = gettmp()
                    rec.dma("sp", lambda e, k=k, j=j, sl=sl, o=t_ap: e.dma_start(out=o, in_=parts_v[j, :, k, sl]),
                            [dr], [t_tk])
                    rec.dve(lambda e, k=k, sl=sl, t=t_ap: e.tensor_tensor(h_sb[:, k, sl], h_sb[:, k, sl], t, ALU.add),
                            [h_tk[k], t_tk], [h_tk[k]])
        h_tiles = [h_sb[:, k, :] for k in range(KT)]
        hn_tiles = [hn_sb[:, k, :] for k in range(KT)]
        rmsnorm_fm(cx, pp, h_tiles, h_tk, g_sb, g_tk, 0, TOK, hn_tiles, hn_tk, ones, ones_tk,
                   [scr[:, 0, :], scr[:, 1, :]], scr_tk, rstd[:, :], rstd_tk)
        ws = WStream(cx, nbuf=2)
        NTB = TOK // 512
        NQ = DFF // (HT * 128)
        for q in range(NQ):
            for gi in range(HT // 4):
                col0 = q * HT * 128 + gi * 512
                wt, wtk = ws.load(wu_d[:, col0:col0 + 512], dr)
                for nn in range(4):
                    hidx = gi * 4 + nn
                    for tb in range(NTB):
                        sl = slice(tb * 512, (tb + 1) * 512)
                        ps, ps_tk = pp.get()
                        for k in range(KT):
                            rec.pe(lambda e, o=ps[:, :], l=wt[:, k, nn * 128:(nn + 1) * 128], r=hn_sb[:, k, sl], k=k:
                                   e.matmul(o, l, r, start=(k == 0), stop=(k == KT - 1)),
                                   [wtk, hn_tk[k]], [ps_tk])
                        t_ap, t_tk = gettmp()
                        rec.act(lambda e, o=t_ap, i=ps[:, :]: e.activation(o, i, AF.Square),
                                [ps_tk], [t_tk])
                        rec.dve(lambda e, o=hid[:, hidx, sl], i=ps[:, :], s=t_ap:
                                e.scalar_tensor_tensor(o, i, 0.0, s, ALU.is_gt, ALU.mult),
                                [ps_tk, t_tk], [hid_tk[hidx][tb]])
            for mg in range(4):
                wt, wtk = ws.load(wd_d[q * HT * 128:(q + 1) * HT * 128, mg * 512:(mg + 1) * 512], dr)
                for mm in range(4):
                    m = mg * 4 + mm
                    for tb in range(NTB):
                        sl = slice(tb * 512, (tb + 1) * 512)
                        ps, ps_tk = pp.get()
                        for k in range(HT):
                            rec.pe(lambda e, o=ps[:, :], l=wt[:, k, mm * 128:(mm + 1) * 128], r=hid[:, k, sl], k=k:
                                   e.matmul(o, l, r, start=(k == 0), stop=(k == HT - 1)),
                                   [wtk, hid_tk[k][tb]], [ps_tk])
                        rec.dve(lambda e, o=h_sb[:, m, sl], i=ps[:, :]: e.tensor_tensor(o, i, o, ALU.add),
                                [ps_tk, h_tk[m]], [h_tk[m]])
        out_tks = []
        h2_v = h2_d.rearrange("(kc p) t -> p kc t", p=128)
        hn_v = hn_d.rearrange("(kc p) t -> p kc t", p=128)
        for k in range(KT):
            otk = Tk()
            out_tks.append(otk)
            rec.dma("sp", lambda e, k=k: e.dma_start(out=h2_v[:, k, :], in_=h_sb[:, k, :]), [h_tk[k]], [otk])
        if final_norm:
            on_sb = cx.sb([128, 2, TOK], F32, "on")
            on_tiles = [on_sb[:, k % 2, :] for k in range(KT)]
            _t = [Tk(), Tk()]
            on_tk = [_t[k % 2] for k in range(KT)]
        else:
            on_tiles = [hn_sb[:, k, :] for k in range(KT)]
            on_tk = hn_tk

        def after(k):
            otk = Tk()
            out_tks.append(otk)
            rec.dma("sp", lambda e, k=k: e.dma_start(out=hn_v[:, k, :], in_=on_tiles[k]), [on_tk[k]], [otk])

        rmsnorm_fm(cx, pp, h_tiles, h_tk, g_sb, g_tk, 16, TOK, on_tiles, on_tk, ones, ones_tk,
                   [scr[:, 0, :], scr[:, 1, :]], scr_tk, rstd[:, :], rstd_tk, after=after)
        rec.add("sp", None, out_tks, [])
        cx.finish()
    return nc


class TempPool:
    def __init__(self, cx, n, shape, dt):
        self.t = [cx.sb(shape, dt) for _ in range(n)]
        self.k = [Tk() for _ in range(n)]
        self.i = 0

    def get(self):
        i = self.i % len(self.t)
        self.i += 1
        return self.t[i], self.k[i]


def load_w_bf16(cx, dram_ap, kt, ncol, dr, name):
    t = cx.sb([128, kt, ncol], BF16, name)
    tk = Tk()
    src = dram_ap.rearrange("(kc p) n -> p kc n", p=128)
    step = max(1, 8192 // ncol)
    for k0 in range(0, kt, step):
        k1 = min(kt, k0 + step)
        cx.rec.dma("pool", lambda e, o=t[:, k0:k1, :], i=src[:, k0:k1, :]: e.dma_start(out=o, in_=i), [dr], [tk])
    return t, tk


GELU_C = 1.5957691216057308


def build_rglru_kernel():
    from contextlib import ExitStack
    nc = bass.Bass("TRN2", target_bir_lowering=False)
    hn_d = nc.dram_tensor("hnT", [D, S], BF16, kind="ExternalInput").ap()
    wg_d = nc.dram_tensor("w_in_g", [D, CH], F32, kind="ExternalInput").ap()
    wr_d = nc.dram_tensor("w_in_r", [D, CH], F32, kind="ExternalInput").ap()
    vec_d = nc.dram_tensor("vec", [128, 4, 8], F32, kind="ExternalInput").ap()
    gx_d = nc.dram_tensor("gx_w", [2, 256, 256], F32, kind="ExternalInput").ap()
    ga_d = nc.dram_tensor("ga_w", [2, 256, 256], F32, kind="ExternalInput").ap()
    wo_d = nc.dram_tensor("w_out", [CH, D], F32, kind="ExternalInput").ap()
    out_d = nc.dram_tensor("partT", [D, S], F32, kind="ExternalOutput").ap()
    NB = S // 512
    with ExitStack() as stack:
        cx = Ctx(nc, stack)
        rec = cx.rec
        dr = Tk()
        ones, ones_tk = setup_consts(cx)
        pp = PsumPool(cx, 8)
        wg, wg_tk = load_w_bf16(cx, wg_d, 16, CH, dr, "wg")
        wr, wr_tk = load_w_bf16(cx, wr_d, 16, CH, dr, "wr")
        wo, wo_tk = load_w_bf16(cx, wo_d, 4, D, dr, "wo")
        gxw = cx.sb([128, 2, 2, 256], BF16, "gxw")
        gaw = cx.sb([128, 2, 2, 256], BF16, "gaw")
        gw_tk = Tk()
        for (dst, src) in ((gxw, gx_d), (gaw, ga_d)):
            for h in range(2):
                rec.dma("pool", lambda e, o=dst[:, h, :, :], i=src[h].rearrange("(kc p) n -> p kc n", p=128):
                        e.dma_start(out=o, in_=i), [dr], [gw_tk])
        vec = cx.sb([128, 4, 8], F32, "vec_sb")
        vec_tk = Tk()
        rec.dma("sp", lambda e: e.dma_start(out=vec[:, :, :], in_=vec_d), [dr], [vec_tk])
        cc = cx.sb([128, 4, 4], F32, "cc")
        cc_tk = Tk()
        one_c = cx.sb([128, 1], F32, "one_c")
        one_tk = Tk()
        rec.dve(lambda e: e.memset(one_c[:, :], 1.0), [], [one_tk])
        for t in range(4):
            rec.act(lambda e, t=t: e.activation(cc[:, t, 0:1], vec[:, t, 7:8], AF.Exp, scale=-1.0), [vec_tk], [cc_tk])
            rec.act(lambda e, t=t: e.activation(cc[:, t, 1:2], cc[:, t, 0:1], AF.Ln, bias=one_c[:, 0:1]),
                    [cc_tk, one_tk], [cc_tk])
            rec.dve(lambda e, t=t: e.tensor_scalar(cc[:, t, 2:3], cc[:, t, 1:2], -8.0, None, ALU.mult), [cc_tk], [cc_tk])
            rec.dve(lambda e, t=t: e.tensor_scalar(cc[:, t, 3:4], cc[:, t, 1:2], -16.0, None, ALU.mult), [cc_tk], [cc_tk])
        hnb = [cx.sb([128, 16, 512], BF16, f"hnb{i}") for i in range(2)]
        hnb_tk = [Tk(), Tk()]
        tp = TempPool(cx, 12, [128, 512], F32)
        recbuf = [cx.sb([128, 3 + 512], F32, f"recbuf{t}") for t in range(4)]
        recbuf_tk = [Tk() for _ in range(4)]
        for t in range(4):
            rec.dve(lambda e, t=t: e.memset(recbuf[t][:, 0:3], 0.0), [], [recbuf_tk[t]])
        hseq = [[cx.sb([128, 512], F32, f"hseq{t}_{i}") for i in range(2)] for t in range(4)]
        hseq_tk = [[Tk(), Tk()] for _ in range(4)]
        rc32 = [cx.sb([128, 512], F32, f"rc32_{t}") for t in range(4)]
        rc32_tk = [Tk() for _ in range(4)]
        rc16 = [cx.sb([128, 512], BF16, f"rc16_{t}") for t in range(4)]
        rc16_tk = [Tk() for _ in range(4)]
        gs = [cx.sb([128, 512], F32, f"gs_{t}") for t in range(4)]
        gs_tk = [Tk() for _ in range(4)]
        ybf = [cx.sb([128, 512], BF16, f"y_{t}") for t in range(4)]
        ybf_tk = [Tk() for _ in range(4)]
        ost = [cx.sb([128, 2, 512], F32, f"ost{i}") for i in range(2)]
        ost_tk = [Tk(), Tk()]
        out_tks = []
        hn_v = hn_d.rearrange("(kc p) t -> p kc t", p=128)
        out_v = out_d.rearrange("(kc p) t -> p kc t", p=128)
        osti = 0
        for tb in range(NB):
            sl = slice(tb * 512, (tb + 1) * 512)
            hb, hb_tk = hnb[tb % 2], hnb_tk[tb % 2]
            rec.dma("sp", lambda e, o=hb[:, :, :], i=hn_v[:, :, sl]: e.dma_start(out=o, in_=i), [dr], [hb_tk])
            for t in range(4):
                ps, ps_tk = pp.get()
                for k in range(16):
                    rec.pe(lambda e, o=ps[:, :], l=wg[:, k, t * 128:(t + 1) * 128], r=hb[:, k, :], k=k:
                           e.matmul(o, l, r, start=(k == 0), stop=(k == 15)), [wg_tk, hb_tk], [ps_tk])
                x, x_tk = tp.get()
                u, u_tk = tp.get()
                rec.act(lambda e, o=x[:, :], i=ps[:, :]: e.activation(o, i, AF.Copy), [ps_tk], [x_tk])
                rec.act(lambda e, o=u[:, :], i=ps[:, :]: e.activation(o, i, AF.Square), [ps_tk], [u_tk])
                rec.dve(lambda e, o=u[:, :]: e.tensor_scalar(o, o, 0.044715, 1.0, ALU.mult, ALU.add), [u_tk], [u_tk])
                rec.dve(lambda e, o=u[:, :], a=x[:, :]: e.tensor_tensor(o, o, a, ALU.mult), [u_tk, x_tk], [u_tk])
                rec.act(lambda e, o=u[:, :]: e.activation(o, o, AF.Sigmoid, scale=GELU_C), [u_tk], [u_tk])
                rec.dve(lambda e, o=gs[t][:, :], a=u[:, :], b=x[:, :]: e.tensor_tensor(o, a, b, ALU.mult),
                        [u_tk, x_tk], [gs_tk[t]])
            for t in range(4):
                ps, ps_tk = pp.get()
                for k in range(16):
                    rec.pe(lambda e, o=ps[:, :], l=wr[:, k, t * 128:(t + 1) * 128], r=hb[:, k, :], k=k:
                           e.matmul(o, l, r, start=(k == 0), stop=(k == 15)), [wr_tk, hb_tk], [ps_tk])
                rb, rb_tk = recbuf[t], recbuf_tk[t]
                rec.act(lambda e, o=rb[:, 3:515], i=ps[:, :]: e.activation(o, i, AF.Copy), [ps_tk], [rb_tk])
                rc, rc_tk = rc32[t], rc32_tk[t]
                rec.dve(lambda e, o=rc[:, :], i=rb[:, 0:512], t=t:
                        e.tensor_scalar(o, i, vec[:, t, 0:1], vec[:, t, 4:5], ALU.mult, ALU.add),
                        [rb_tk, vec_tk], [rc_tk])
                for k in range(1, 4):
                    rec.dve(lambda e, o=rc[:, :], i=rb[:, k:k + 512], t=t, k=k:
                            e.scalar_tensor_tensor(o, i, vec[:, t, k:k + 1], o, ALU.mult, ALU.add),
                            [rb_tk, vec_tk, rc_tk], [rc_tk])
                rec.act(lambda e, o=rc16[t][:, :], i=rc[:, :]: e.activation(o, i, AF.Copy), [rc_tk], [rc16_tk[t]])
                rec.pool(lambda e, o=rb[:, 0:3], i=rb[:, 512:515]: e.tensor_copy(o, i), [rb_tk], [rb_tk])
            for t in range(4):
                h = t // 2
                nn = t % 2
                psx, psx_tk = pp.get()
                psa, psa_tk = pp.get()
                for (pso, pso_tk, w) in ((psx, psx_tk, gxw), (psa, psa_tk, gaw)):
                    for k in range(2):
                        rec.pe(lambda e, o=pso[:, :], l=w[:, h, k, nn * 128:(nn + 1) * 128], r=rc16[2 * h + k][:, :], k=k:
                               e.matmul(o, l, r, start=(k == 0), stop=(k == 1)),
                               [gw_tk, rc16_tk[2 * h + k]], [pso_tk])
                it, it_tk = tp.get()
                at, at_tk = tp.get()
                a2, a2_tk = tp.get()
                rec.act(lambda e, o=it[:, :], i=psx[:, :], t=t: e.activation(o, i, AF.Sigmoid, bias=vec[:, t, 5:6]),
                        [psx_tk, vec_tk], [it_tk])
                rec.act(lambda e, o=a2[:, :], i=psa[:, :], t=t: e.activation(o, i, AF.Sigmoid, bias=vec[:, t, 6:7]),
                        [psa_tk, vec_tk], [a2_tk])
                rec.act(lambda e, o=at[:, :], i=a2[:, :], t=t: e.activation(o, i, AF.Exp, scale=cc[:, t, 2:3]),
                        [a2_tk, cc_tk], [at_tk])
                rec.act(lambda e, o=a2[:, :], t=t: e.activation(o, o, AF.Exp, scale=cc[:, t, 3:4]),
                        [a2_tk, cc_tk], [a2_tk])
                rec.act(lambda e, o=a2[:, :]: e.activation(o, o, AF.Sqrt, scale=-1.0, bias=one_c[:, 0:1]),
                        [a2_tk, one_tk], [a2_tk])
                rec.dve(lambda e, o=it[:, :], b=rc32[t][:, :]: e.tensor_tensor(o, o, b, ALU.mult), [it_tk, rc32_tk[t]], [it_tk])
                rec.dve(lambda e, o=it[:, :], b=a2[:, :]: e.tensor_tensor(o, o, b, ALU.mult), [it_tk, a2_tk], [it_tk])
                hs, hs_tk = hseq[t][tb % 2], hseq_tk[t][tb % 2]
                hp, hp_tk = hseq[t][(tb + 1) % 2], hseq_tk[t][(tb + 1) % 2]
                if tb == 0:
                    rec.dve(lambda e, o=hs[:, :], a=at[:, :], b=it[:, :]:
                            e.tensor_tensor_scan(o, a, b, 0.0, ALU.mult, ALU.add), [at_tk, it_tk], [hs_tk])
                else:
                    rec.dve(lambda e, o=hs[:, :], a=at[:, :], b=it[:, :], ini=hp[:, 511:512]:
                            e.tensor_tensor_scan(o, a, b, ini, ALU.mult, ALU.add), [at_tk, it_tk, hp_tk], [hs_tk])
                rec.dve(lambda e, o=ybf[t][:, :], a=hs[:, :], b=gs[t][:, :]: e.tensor_tensor(o, a, b, ALU.mult),
                        [hs_tk, gs_tk[t]], [ybf_tk[t]])
            for m2 in range(8):
                os_, os_tk = ost[osti % 2], ost_tk[osti % 2]
                osti += 1
                for mm in range(2):
                    m = m2 * 2 + mm
                    ps, ps_tk = pp.get()
                    for k in range(4):
                        rec.pe(lambda e, o=ps[:, :], l=wo[:, k, m * 128:(m + 1) * 128], r=ybf[k][:, :], k=k:
                               e.matmul(o, l, r, start=(k == 0), stop=(k == 3)), [wo_tk, ybf_tk[k]], [ps_tk])
                    if mm == 0:
                        rec.act(lambda e, o=os_[:, mm, :], i=ps[:, :]: e.activation(o, i, AF.Copy), [ps_tk], [os_tk])
                    else:
                        rec.dve(lambda e, o=os_[:, mm, :], i=ps[:, :]: e.tensor_copy(o, i), [ps_tk], [os_tk])
                otk = Tk()
                out_tks.append(otk)
                rec.dma("sp", lambda e, o=out_v[:, m2 * 2:m2 * 2 + 2, sl], i=os_[:, :, :]: e.dma_start(out=o, in_=i),
                        [os_tk], [otk])
        rec.add("sp", None, out_tks, [])
        cx.finish()
    return nc


def vlay(v, g):
    return np.ascontiguousarray(np.asarray(v)[CH * g:CH * g + CH].reshape(4, 128).T)


def rglru_inputs(g, hnT_bf, w_in, conv_w, conv_b, gx_w, gx_b, ga_w, ga_b, lam, w_out):
    vec = np.zeros((128, 4, 8), np.float32)
    for k in range(4):
        vec[:, :, k] = vlay(conv_w[k], g)
    vec[:, :, 4] = vlay(conv_b, g)
    vec[:, :, 5] = vlay(gx_b, g)
    vec[:, :, 6] = vlay(ga_b, g)
    vec[:, :, 7] = vlay(lam, g)
    return {
        "hnT": hnT_bf,
        "w_in_g": np.ascontiguousarray(w_in[:, CH * g:CH * g + CH]),
        "w_in_r": np.ascontiguousarray(w_in[:, D + CH * g:D + CH * g + CH]),
        "vec": vec,
        "gx_w": np.ascontiguousarray(gx_w[2 * g:2 * g + 2]),
        "ga_w": np.ascontiguousarray(ga_w[2 * g:2 * g + 2]),
        "w_out": np.ascontiguousarray(w_out[CH * g:CH * g + CH, :]),
    }


def build_rwkv1_kernel():
    from contextlib import ExitStack
    nc = bass.Bass("TRN2", target_bir_lowering=False)
    hn_d = nc.dram_tensor("hnT", [D, S], BF16, kind="ExternalInput").ap()
    mu_d = nc.dram_tensor("mu", [128, 16, 6], F32, kind="ExternalInput").ap()
    wrkv_d = [nc.dram_tensor(f"w_rkv{j}", [D, CH], F32, kind="ExternalInput").ap() for j in range(3)]
    w1_d = nc.dram_tensor("w1", [D, 96], F32, kind="ExternalInput").ap()
    a1_d = nc.dram_tensor("a1", [D, 96], F32, kind="ExternalInput").ap()
    g1_d = nc.dram_tensor("g1", [D, 256], F32, kind="ExternalInput").ap()
    w2_d = nc.dram_tensor("w2", [96, CH], F32, kind="ExternalInput").ap()
    a2_d = nc.dram_tensor("a2", [96, CH], F32, kind="ExternalInput").ap()
    g2_d = nc.dram_tensor("g2", [256, CH], F32, kind="ExternalInput").ap()
    vec_d = nc.dram_tensor("vec", [128, 4, 8], F32, kind="ExternalInput").ap()
    proj_d = nc.dram_tensor("proj", [6, CH, S], F32, kind="ExternalOutput").ap()
    NB = S // 512
    with ExitStack() as stack:
        cx = Ctx(nc, stack)
        rec = cx.rec
        dr = Tk()
        pp = PsumPool(cx, 8)
        wj = []
        for j in range(3):
            wj.append(load_w_bf16(cx, wrkv_d[j], 16, CH, dr, f"wj{j}"))
        w1, w1_tk = load_w_bf16(cx, w1_d, 16, 96, dr, "w1s")
        a1, a1_tk = load_w_bf16(cx, a1_d, 16, 96, dr, "a1s")
        g1, g1_tk = load_w_bf16(cx, g1_d, 16, 256, dr, "g1s")
        g2, g2_tk = load_w_bf16(cx, g2_d, 2, CH, dr, "g2s")
        w2 = cx.sb([128, CH], BF16, "w2s")
        a2 = cx.sb([128, CH], BF16, "a2s")
        w2_tk = Tk()
        rec.dma("pool", lambda e: e.dma_start(out=w2[0:96, :], in_=w2_d), [dr], [w2_tk])
        rec.dma("pool", lambda e: e.dma_start(out=a2[0:96, :], in_=a2_d), [dr], [w2_tk])
        vec = cx.sb([128, 4, 8], F32, "vec_sb")
        mu = cx.sb([128, 16, 6], F32, "mu_sb")
        vec_tk = Tk()
        rec.dma("sp", lambda e: e.dma_start(out=vec[:, :, :], in_=vec_d), [dr], [vec_tk])
        rec.dma("sp", lambda e: e.dma_start(out=mu[:, :, :], in_=mu_d), [dr], [vec_tk])
        hb = cx.sb([128, 16, 513], BF16, "hb")
        hb_tk = Tk()
        rec.dve(lambda e: e.memset(hb[:, :, 0:1], 0.0), [], [hb_tk])
        xx = cx.sb([128, 16, 512], BF16, "xx")
        xx_tk = Tk()
        xm = [cx.sb([128, 16, 512], BF16, f"xm{i}") for i in range(2)]
        xm_tk = [Tk(), Tk()]
        mid = [cx.sb([128, 512], BF16, f"mid{i}") for i in range(3)]
        mid_tk = [Tk() for _ in range(3)]
        st = TempPool(cx, 6, [128, 512], F32)
        hn_v = hn_d.rearrange("(kc p) t -> p kc t", p=128)
        proj_v = proj_d.rearrange("q (t p) s -> q p t s", p=128)
        out_tks = []

        def emit_out(q, t, sl, ps, ps_tk, func, bias=None):
            s_, s_tk = st.get()
            if bias is None:
                rec.act(lambda e, o=s_[:, :], i=ps[:, :]: e.activation(o, i, func), [ps_tk], [s_tk])
            else:
                rec.act(lambda e, o=s_[:, :], i=ps[:, :], b=bias: e.activation(o, i, func, bias=b), [ps_tk, vec_tk], [s_tk])
            otk = Tk()
            out_tks.append(otk)
            rec.dma("sp", lambda e, o=proj_v[q, :, t, sl], i=s_[:, :]: e.dma_start(out=o, in_=i), [s_tk], [otk])

        for tb in range(NB):
            sl = slice(tb * 512, (tb + 1) * 512)
            if tb == 0:
                rec.dma("sp", lambda e: e.dma_start(out=hb[:, :, 1:513], in_=hn_v[:, :, 0:512]), [dr], [hb_tk])
            else:
                rec.dma("sp", lambda e, a=tb * 512 - 1: e.dma_start(out=hb[:, :, :], in_=hn_v[:, :, a:a + 513]), [dr], [hb_tk])
            rec.dve(lambda e: e.tensor_tensor(xx[:, :, :], hb[:, :, 0:512], hb[:, :, 1:513], ALU.subtract), [hb_tk], [xx_tk])
            for j in range(6):
                xb, xb_tk = xm[j % 2], xm_tk[j % 2]
                for k in range(16):
                    rec.dve(lambda e, o=xb[:, k, :], i=xx[:, k, :], m=mu[:, k, j:j + 1], h=hb[:, k, 1:513]:
                            e.scalar_tensor_tensor(o, i, m, h, ALU.mult, ALU.add), [xx_tk, hb_tk, vec_tk], [xb_tk])
                if j < 3:
                    w, w_tk = wj[j]
                    for t in range(4):
                        ps, ps_tk = pp.get()
                        for k in range(16):
                            rec.pe(lambda e, o=ps[:, :], l=w[:, k, t * 128:(t + 1) * 128], r=xb[:, k, :], k=k:
                                   e.matmul(o, l, r, start=(k == 0), stop=(k == 15)), [w_tk, xb_tk], [ps_tk])
                        emit_out(j, t, sl, ps, ps_tk, AF.Copy)
                elif j < 5:
                    wA, wA_tk = (w1, w1_tk) if j == 3 else (a1, a1_tk)
                    wB = w2 if j == 3 else a2
                    ps, ps_tk = pp.get()
                    for k in range(16):
                        rec.pe(lambda e, o=ps[0:96, :], l=wA[:, k, :], r=xb[:, k, :], k=k:
                               e.matmul(o, l, r, start=(k == 0), stop=(k == 15)), [wA_tk, xb_tk], [ps_tk])
                    md, md_tk = mid[j - 3], mid_tk[j - 3]
                    rec.act(lambda e, o=md[0:96, :], i=ps[0:96, :], f=(AF.Tanh if j == 3 else AF.Copy): e.activation(o, i, f),
                            [ps_tk], [md_tk])
                    for t in range(4):
                        ps2, ps2_tk = pp.get()
                        rec.pe(lambda e, o=ps2[:, :], l=wB[0:96, t * 128:(t + 1) * 128], r=md[0:96, :]:
                               e.matmul(o, l, r, start=True, stop=True), [w2_tk, md_tk], [ps2_tk])
                        emit_out(j, t, sl, ps2, ps2_tk, AF.Sigmoid, bias=vec[:, t, (0 if j == 3 else 1):(1 if j == 3 else 2)])
                else:
                    sg_ = []
                    for c in range(2):
                        ps, ps_tk = pp.get()
                        for k in range(16):
                            rec.pe(lambda e, o=ps[:, :], l=g1[:, k, c * 128:(c + 1) * 128], r=xb[:, k, :], k=k:
                                   e.matmul(o, l, r, start=(k == 0), stop=(k == 15)), [g1_tk, xb_tk], [ps_tk])
                        md, md_tk = mid[c], mid_tk[c]
                        rec.act(lambda e, o=md[:, :], i=ps[:, :]: e.activation(o, i, AF.Sigmoid), [ps_tk], [md_tk])
                        sg_.append((md, md_tk))
                    for t in range(4):
                        ps2, ps2_tk = pp.get()
                        for c in range(2):
                            rec.pe(lambda e, o=ps2[:, :], l=g2[:, c, t * 128:(t + 1) * 128], r=sg_[c][0][:, :], c=c:
                                   e.matmul(o, l, r, start=(c == 0), stop=(c == 1)), [g2_tk, sg_[c][1]], [ps2_tk])
                        emit_out(5, t, sl, ps2, ps2_tk, AF.Copy)
        rec.add("sp", None, out_tks, [])
        cx.finish()
    return nc


def rwkv_vec(g, w0, a0, k_k, k_a, r_k, ln_g, ln_b):
    vec = np.zeros((128, 4, 8), np.float32)
    for i, v in enumerate((w0, a0, k_k, k_a, np.asarray(r_k).reshape(-1), ln_g, ln_b)):
        vec[:, :, i] = vlay(v, g)
    return vec


def rwkv1_inputs(g, hnT_bf, mu, w_rkv, w1, a1, g1, w2, a2, g2, vec):
    sl = slice(CH * g, CH * g + CH)
    d = {"hnT": hnT_bf,
         "mu": np.ascontiguousarray(np.asarray(mu).reshape(6, 16, 128).transpose(2, 1, 0)),
         "w1": np.ascontiguousarray(w1), "a1": np.ascontiguousarray(a1), "g1": np.ascontiguousarray(g1),
         "w2": np.ascontiguousarray(w2[:, sl]), "a2": np.ascontiguousarray(a2[:, sl]),
         "g2": np.ascontiguousarray(g2[:, sl]), "vec": vec}
    for j in range(3):
        d[f"w_rkv{j}"] = np.ascontiguousarray(w_rkv[j][:, sl])
    return d


C0 = float(np.exp(-0.5))
GN_EPS = 64e-5


def rwkv2_consts():
    p = np.arange(128)
    bo = (p[:, None] // 64 == p[None, :] // 64).astype(np.float32)
    ident = np.eye(128, dtype=np.float32)
    cm = np.ones((128, 256), np.float32)
    cm[:, ::64] = 0.0
    am = np.zeros((128, 512), np.float32)
    r = p[:, None] % 64
    c = p[None, :] % 64
    am[:, 0:128] = -1.0 * bo * (r < c)
    am[:, 128:256] = -1.0 * bo * (r > c)
    am[:, 256:384] = bo * (r < c)
    st = (p[:, None] % 64 <= np.arange(64)[None, :]).astype(np.float32)
    am[:, 384:448] = st
    am[:, 448:512] = st
    return {"bo": bo.astype(ml_dtypes.bfloat16), "ident": ident.astype(ml_dtypes.bfloat16), "cmask": cm, "ammask": am}


def build_rwkv2_kernel():
    from contextlib import ExitStack
    nc = bass.Bass("TRN2", target_bir_lowering=False)
    proj_d = nc.dram_tensor("proj", [6, CH, S], F32, kind="ExternalInput").ap()
    vec_d = nc.dram_tensor("vec", [128, 4, 8], F32, kind="ExternalInput").ap()
    bo_d = nc.dram_tensor("bo", [128, 128], BF16, kind="ExternalInput").ap()
    id_d = nc.dram_tensor("ident", [128, 128], BF16, kind="ExternalInput").ap()
    cm_d = nc.dram_tensor("cmask", [128, 256], F32, kind="ExternalInput").ap()
    am_d = nc.dram_tensor("ammask", [128, 512], F32, kind="ExternalInput").ap()
    wo_d = nc.dram_tensor("w_out", [CH, D], F32, kind="ExternalInput").ap()
    out_d = nc.dram_tensor("partT", [D, S], F32, kind="ExternalOutput").ap()
    BLK = 256
    NCH = BLK // 64
    NB = S // BLK
    with ExitStack() as stack:
        cx = Ctx(nc, stack)
        rec = cx.rec
        dr = Tk()
        pp = PsumPool(cx, 6)
        pst2 = [cx.ps([128, 1024], BF16, f"pst{i}") for i in range(2)]
        pst_tk = [Tk(), Tk()]
        wo, wo_tk = load_w_bf16(cx, wo_d, 4, D, dr, "wo")
        cst_tk = Tk()
        bo = cx.sb([128, 128], BF16, "bo_sb")
        ident = cx.sb([128, 128], BF16, "id_sb")
        cmask = cx.sb([128, 256], F32, "cm_sb")
        ammask = cx.sb([128, 512], F32, "am_sb")
        vec = cx.sb([128, 4, 8], F32, "vec_sb")
        for (o_, i_) in ((bo, bo_d), (ident, id_d), (cmask, cm_d), (ammask, am_d)):
            rec.dma("sp", lambda e, o=o_, i=i_: e.dma_start(out=o[:, :], in_=i), [dr], [cst_tk])
        rec.dma("sp", lambda e: e.dma_start(out=vec[:, :, :], in_=vec_d), [dr], [cst_tk])
        gne = cx.sb([128, 1], F32, "gne")
        rec.dve(lambda e: e.memset(gne[:, :], GN_EPS), [], [cst_tk])
        inp = cx.sb([128, 6, 4, BLK], F32, "inp")
        inp_tk = [Tk() for _ in range(6)]
        tp = TempPool(cx, 12, [128, BLK], F32)
        tpb = TempPool(cx, 4, [128, BLK], BF16)
        bonus = [cx.sb([128, BLK], F32, f"bonus{t}") for t in range(4)]
        bonus_tk = [Tk() for _ in range(4)]
        eG = [cx.sb([128, NCH, 64], F32, f"eG{t}") for t in range(4)]
        eG_tk = [Tk() for _ in range(4)]
        rt = [cx.sb([128, BLK], BF16, f"rt{t}") for t in range(4)]
        rt_tk = [Tk() for _ in range(4)]
        y32 = [cx.sb([128, BLK], F32, f"y32_{t}") for t in range(4)]
        y32_tk = [Tk() for _ in range(4)]
        yo = [cx.sb([128, BLK], BF16, f"yo{t}") for t in range(4)]
        yo_tk = [Tk() for _ in range(4)]
        BD = ("kt", "bt", "p", "kE", "bE", "v")
        bd = {}
        bd_tk = {}
        for nm in BD:
            for t in range(4):
                bd[nm, t] = cx.sb([128, NCH, 128], BF16, f"bd_{nm}{t}")
                bd_tk[nm, t] = Tk()
                rec.pool(lambda e, o=bd[nm, t]: e.memset(o[:, :, :], 0.0), [], [bd_tk[nm, t]])
        NU = 4 * NCH
        TM = [cx.sb([128, 3, 128], BF16, f"TM{u}") for u in range(NU)]
        TM_tk = [Tk() for _ in range(NU)]
        ub = [cx.sb([128, 1024], BF16, f"ub{u}") for u in range(NU)]
        ua_tk = [Tk() for _ in range(NU)]
        ub_tk = [Tk() for _ in range(NU)]
        uam_tk = [Tk() for _ in range(NU)]
        uz_tk = [Tk() for _ in range(NU)]
        T32 = [cx.sb([128, 128], F32, f"T32_{t}") for t in range(4)]
        T16 = [cx.sb([128, 128], BF16, f"T16_{t}") for t in range(4)]
        T32_tk = [Tk() for _ in range(4)]
        T16_tk = [Tk() for _ in range(4)]
        WA = [cx.sb([128, 128], BF16, f"WA{t}") for t in range(4)]
        WA_tk = [Tk() for _ in range(4)]
        Ub = [cx.sb([128, 128], BF16, f"Ub{t}") for t in range(4)]
        Ub_tk = [Tk() for _ in range(4)]
        for t in range(4):
            rec.dve(lambda e, o=T32[t]: e.memset(o[:, :], 0.0), [], [T32_tk[t]])
            rec.dve(lambda e, o=T16[t]: e.memset(o[:, :], 0.0), [], [T16_tk[t]])
        ost = [cx.sb([128, 4, BLK], F32, f"ost{i}") for i in range(2)]
        ost_tk = [Tk(), Tk()]
        osti = 0
        out_tks = []
        proj_v = proj_d.rearrange("q (t p) s -> q p t s", p=128)
        out_v = out_d.rearrange("(kc p) t -> p kc t", p=128)

        def c3(ap2d):
            return ap2d.rearrange("p (c s) -> p c s", s=64)

        for tb in range(NB):
            sl = slice(tb * BLK, (tb + 1) * BLK)
            for q in range(6):
                rec.dma("sp", lambda e, q=q, sl=sl: e.dma_start(out=inp[:, q, :, :], in_=proj_v[q, :, :, sl]),
                        [dr], [inp_tk[q]])
            for t in range(4):
                r32, k32, v32, sg, aa = (inp[:, q, t, :] for q in range(5))
                kk0, kk0_tk = tp.get()
                rec.dve(lambda e, o=kk0[:, :], i=k32, t=t: e.tensor_scalar(o, i, vec[:, t, 2:3], None, ALU.mult),
                        [inp_tk[1], cst_tk], [kk0_tk])
                sq, sq_tk = tpb.get()
                rec.act(lambda e, o=sq[:, :], i=kk0[:, :]: e.activation(o, i, AF.Square), [kk0_tk], [sq_tk])
                ps, ps_tk = pp.get()
                rec.pe(lambda e, o=ps[:, 0:BLK], r=sq[:, :]: e.matmul(o, bo[:, :], r, start=True, stop=True),
                       [sq_tk, cst_tk], [ps_tk])
                nrm, nrm_tk = tp.get()
                rec.act(lambda e, o=nrm[:, :], i=ps[:, 0:BLK]: e.activation(o, i, AF.Sqrt), [ps_tk], [nrm_tk])
                rec.dve(lambda e, o=nrm[:, :]: e.tensor_scalar(o, o, 1e-12, None, ALU.max), [nrm_tk], [nrm_tk])
                rec.dve(lambda e, o=nrm[:, :]: e.reciprocal(o, o), [nrm_tk], [nrm_tk])
                rec.dve(lambda e, o=kk0[:, :], b=nrm[:, :]: e.tensor_tensor(o, o, b, ALU.mult), [kk0_tk, nrm_tk], [kk0_tk])
                kk, kk_tk = kk0, kk0_tk
                km, km_tk = tp.get()
                rec.dve(lambda e, o=km[:, :], i=aa, t=t: e.tensor_scalar(o, i, 1.0, vec[:, t, 3:4], ALU.subtract, ALU.mult),
                        [inp_tk[4], cst_tk], [km_tk])
                rec.dve(lambda e, o=km[:, :], b=k32: e.scalar_tensor_tensor(o, o, 1.0, b, ALU.add, ALU.mult),
                        [km_tk, inp_tk[1]], [km_tk])
                bb, bb_tk = tp.get()
                rec.dve(lambda e, o=bb[:, :], a=kk[:, :], b=aa: e.tensor_tensor(o, a, b, ALU.mult),
                        [kk_tk, inp_tk[4]], [bb_tk])
                rk, rk_tk = tpb.get()
                rec.dve(lambda e, o=rk[:, :], i=r32, b=km[:, :], t=t:
                        e.scalar_tensor_tensor(o, i, vec[:, t, 4:5], b, ALU.mult, ALU.mult),
                        [inp_tk[0], km_tk, cst_tk], [rk_tk])
                ps, ps_tk = pp.get()
                rec.pe(lambda e, o=ps[:, 0:BLK], r=rk[:, :]: e.matmul(o, bo[:, :], r, start=True, stop=True),
                       [rk_tk, cst_tk], [ps_tk])
                rec.dve(lambda e, o=bonus[t][:, :], i=ps[:, 0:BLK], b=v32: e.tensor_tensor(o, i, b, ALU.mult),
                        [ps_tk, inp_tk[2]], [bonus_tk[t]])
                cs, cs_tk = tp.get()
                rec.dve(lambda e, o=cs[:, :], b=sg: e.tensor_tensor_scan(o, cmask[:, :], b, 0.0, ALU.mult, ALU.add),
                        [inp_tk[3], cst_tk], [cs_tk])
                eGf = eG[t][:, :, :].rearrange("p c s -> p (c s)")
                rec.act(lambda e, o=eGf, i=cs[:, :]: e.activation(o, i, AF.Exp, scale=-C0), [cs_tk], [eG_tk[t]])
                enG, enG_tk = tp.get()
                rec.act(lambda e, o=enG[:, :], i=cs[:, :]: e.activation(o, i, AF.Exp, scale=C0), [cs_tk], [enG_tk])
                eGx, eGx_tk = tp.get()
                rec.dve(lambda e, o=eGx[:, :], a=cs[:, :], b=sg: e.tensor_tensor(o, a, b, ALU.subtract),
                        [cs_tk, inp_tk[3]], [eGx_tk])
                rec.act(lambda e, o=eGx[:, :]: e.activation(o, o, AF.Exp, scale=-C0), [eGx_tk], [eGx_tk])
                eE, eE_tk = tp.get()
                csv = c3(cs[:, :])
                rec.dve(lambda e, o=c3(eE[:, :]), a=csv[:, :, 63:64].broadcast_to([128, NCH, 64]), b=csv:
                        e.tensor_tensor(o, a, b, ALU.subtract), [cs_tk], [eE_tk])
                rec.act(lambda e, o=eE[:, :]: e.activation(o, o, AF.Exp, scale=-C0), [eE_tk], [eE_tk])
                rec.dve(lambda e, o=rt[t][:, :], a=r32, b=eGf: e.tensor_tensor(o, a, b, ALU.mult),
                        [inp_tk[0], eG_tk[t]], [rt_tk[t]])
                for (nm, A, A_tk, Bm, B_tk) in (("kt", km, km_tk, enG, enG_tk), ("bt", bb, bb_tk, enG, enG_tk),
                                                ("p", kk, kk_tk, eGx, eGx_tk), ("kE", km, km_tk, eE, eE_tk),
                                                ("bE", bb, bb_tk, eE, eE_tk)):
                    for h in range(2):
                        ps_ = slice(64 * h, 64 * h + 64)
                        rec.pool(lambda e, o=bd[nm, t][ps_, :, 64 * h:64 * h + 64], a=c3(A[ps_, :]), b=c3(Bm[ps_, :]):
                                 e.tensor_tensor(o, a, b, ALU.mult), [A_tk, B_tk], [bd_tk[nm, t]])
                for h in range(2):
                    ps_ = slice(64 * h, 64 * h + 64)
                    rec.pool(lambda e, o=bd["v", t][ps_, :, 64 * h:64 * h + 64], a=c3(inp[ps_, 2, t, :]):
                             e.tensor_copy(o, a), [inp_tk[2]], [bd_tk["v", t]])
            units = [(t, c) for c in range(NCH) for t in range(4)]
            for ui, (t, c) in enumerate(units):
                u = ui
                half = ui % 2
                pst = pst2[half]
                pt = pst[:, 0:384]
                for i3, nm in enumerate(("v", "kE", "bE")):
                    rec.pe(lambda e, o=pst[:, i3 * 128:(i3 + 1) * 128], i=bd[nm, t][:, c, :]:
                           e.transpose(o, i, ident[:, :]), [bd_tk[nm, t], cst_tk], [pst_tk[half]])
                rec.act(lambda e, o=TM[u][:, :, :].rearrange("p a b -> p (a b)"), i=pt: e.activation(o, i, AF.Copy),
                        [pst_tk[half]], [TM_tk[u]])
                ps, ps_tk = pp.get()
                mm = ((0, "bt", "p"), (128, "p", "bt"), (256, "kt", "p"))
                for (c0_, l, r) in mm:
                    rec.pe(lambda e, o=ps[:, c0_:c0_ + 128], l_=bd[l, t][:, c, :], r_=bd[r, t][:, c, :]:
                           e.matmul(o, l_, r_, start=True, stop=True), [bd_tk[l, t], bd_tk[r, t]], [ps_tk])
                for (c0_, l) in ((384, "kt"), (448, "bt")):
                    rec.pe(lambda e, o=ps[:, c0_:c0_ + 64], l_=bd[l, t][:, c, :], r_=rt[t][:, c * 64:(c + 1) * 64]:
                           e.matmul(o, l_, r_, start=True, stop=True), [bd_tk[l, t], rt_tk[t]], [ps_tk])
                rec.dve(lambda e, o=ub[u][:, 0:512], i=ps[:, :]: e.tensor_tensor(o, i, ammask[:, :], ALU.mult),
                        [ps_tk, cst_tk], [ua_tk[u], uam_tk[u]])
                rec.dve(lambda e, o=ub[u][:, 768:896], a=ub[u][:, 0:128]: e.tensor_tensor(o, a, ident[:, :], ALU.add),
                        [ua_tk[u], cst_tk], [uz_tk[u]])
            for k in range(1, 6):
                s0, d0 = (0, 512) if k % 2 == 1 else (512, 0)
                for u in range(NU):
                    s_tk, d_tk = (ua_tk[u], ub_tk[u]) if k % 2 == 1 else (ub_tk[u], ua_tk[u])
                    ps, ps_tk = pp.get()
                    Ms, MTs = ub[u][:, s0:s0 + 128], ub[u][:, s0 + 128:s0 + 256]
                    if k < 5:
                        rec.pe(lambda e, o=ps[:, 0:128], l=MTs, r=Ms: e.matmul(o, l, r, start=True, stop=True), [s_tk], [ps_tk])
                    rec.pe(lambda e, o=ps[:, 128:256], l=Ms, r=MTs: e.matmul(o, l, r, start=True, stop=True), [s_tk], [ps_tk])
                    if k < 5:
                        rec.act(lambda e, o=ub[u][:, d0:d0 + 256], i=ps[:, 0:256]: e.activation(o, i, AF.Copy), [ps_tk], [d_tk])
                    else:
                        rec.act(lambda e, o=ub[u][:, d0 + 128:d0 + 256], i=ps[:, 128:256]: e.activation(o, i, AF.Copy),
                                [ps_tk], [d_tk])
                for u in range(NU):
                    d_tk = ub_tk[u] if k % 2 == 1 else ua_tk[u]
                    ps, ps_tk = pp.get()
                    Z = ub[u][:, 768:896]
                    rec.pe(lambda e, o=ps[:, 0:128], l=ub[u][:, d0 + 128:d0 + 256], r=Z: e.matmul(o, l, r, start=True, stop=True),
                           [d_tk, uz_tk[u]], [ps_tk])
                    if k < 5:
                        rec.dve(lambda e, o=Z, i=ps[:, 0:128]: e.tensor_tensor(o, i, o, ALU.add), [ps_tk, uz_tk[u]], [uz_tk[u]])
                    else:
                        rec.dve(lambda e, o=Z, i=ps[:, 0:128]: e.scalar_tensor_tensor(o, i, -1.0, o, ALU.mult, ALU.subtract),
                                [ps_tk, uz_tk[u]], [uz_tk[u]])
            for ui, (t, c) in enumerate(units):
                u = ui
                Vt, KEt, BEt = TM[u][:, 0, :], TM[u][:, 1, :], TM[u][:, 2, :]
                ps, ps_tk = pp.get()
                rec.pe(lambda e, o=ps[:, 0:128], l=bd["p", t][:, c, :], r=T16[t][:, :]: e.matmul(o, l, r, start=True, stop=False),
                       [bd_tk["p", t], T16_tk[t]], [ps_tk])
                rec.pe(lambda e, o=ps[:, 0:128], l=ub[u][:, 256:384], r=Vt: e.matmul(o, l, r, start=False, stop=True),
                       [uam_tk[u], TM_tk[u]], [ps_tk])
                rec.act(lambda e, o=WA[t][:, :], i=ps[:, 0:128]: e.activation(o, i, AF.Copy), [ps_tk], [WA_tk[t]])
                ps2, ps2_tk = pp.get()
                rec.pe(lambda e, o=ps2[:, 0:128], l=ub[u][:, 768:896], r=WA[t][:, :]: e.matmul(o, l, r, start=True, stop=True),
                       [uz_tk[u], WA_tk[t]], [ps2_tk])
                rec.dve(lambda e, o=Ub[t][:, :], i=ps2[:, 0:128]: e.tensor_copy(o, i), [ps2_tk], [Ub_tk[t]])
                ps3, ps3_tk = pp.get()
                rec.pe(lambda e, o=ps3[:, 0:64], l=T16[t][:, :], r=rt[t][:, c * 64:(c + 1) * 64]:
                       e.matmul(o, l, r, start=True, stop=False), [T16_tk[t], rt_tk[t]], [ps3_tk])
                rec.pe(lambda e, o=ps3[:, 0:64], l=Ub[t][:, :], r=ub[u][:, 448:512]: e.matmul(o, l, r, start=False, stop=False),
                       [Ub_tk[t], uam_tk[u]], [ps3_tk])
                rec.pe(lambda e, o=ps3[:, 0:64], l=Vt, r=ub[u][:, 384:448]: e.matmul(o, l, r, start=False, stop=True),
                       [TM_tk[u], uam_tk[u]], [ps3_tk])
                rec.act(lambda e, o=y32[t][:, c * 64:(c + 1) * 64], i=ps3[:, 0:64]: e.activation(o, i, AF.Copy),
                        [ps3_tk], [y32_tk[t]])
                ps4, ps4_tk = pp.get()
                rec.pe(lambda e, o=ps4[:, 0:128], l=KEt, r=Vt: e.matmul(o, l, r, start=True, stop=False), [TM_tk[u]], [ps4_tk])
                rec.pe(lambda e, o=ps4[:, 0:128], l=BEt, r=Ub[t][:, :]: e.matmul(o, l, r, start=False, stop=True),
                       [TM_tk[u], Ub_tk[t]], [ps4_tk])
                rec.dve(lambda e, o=T32[t][:, :], g_=eG[t][:, c, 63:64], i=ps4[:, 0:128]:
                        e.scalar_tensor_tensor(o, o, g_, i, ALU.mult, ALU.add), [T32_tk[t], eG_tk[t], ps4_tk], [T32_tk[t]])
                rec.act(lambda e, o=T16[t][:, :], i=T32[t][:, :]: e.activation(o, i, AF.Copy), [T32_tk[t]], [T16_tk[t]])
            for t in range(4):
                y16, y16_tk = tpb.get()
                ysq, ysq_tk = tpb.get()
                rec.act(lambda e, o=y16[:, :], i=y32[t][:, :]: e.activation(o, i, AF.Copy), [y32_tk[t]], [y16_tk])
                rec.act(lambda e, o=ysq[:, :], i=y32[t][:, :]: e.activation(o, i, AF.Square), [y32_tk[t]], [ysq_tk])
                ps, ps_tk = pp.get()
                rec.pe(lambda e, o=ps[:, 0:BLK], r=y16[:, :]: e.matmul(o, bo[:, :], r, start=True, stop=True), [y16_tk, cst_tk], [ps_tk])
                rec.pe(lambda e, o=ps[:, BLK:2 * BLK], r=ysq[:, :]: e.matmul(o, bo[:, :], r, start=True, stop=True),
                       [ysq_tk, cst_tk], [ps_tk])
                mean, mean_tk = tp.get()
                var, var_tk = tp.get()
                rec.act(lambda e, o=mean[:, :], i=ps[:, 0:BLK]: e.activation(o, i, AF.Copy, scale=1.0 / 64), [ps_tk], [mean_tk])
                rec.dve(lambda e, o=var[:, :], a=mean[:, :]: e.tensor_tensor(o, a, a, ALU.mult), [mean_tk], [var_tk])
                rec.dve(lambda e, o=var[:, :], i=ps[:, BLK:2 * BLK]: e.scalar_tensor_tensor(o, i, 1.0 / 64, o, ALU.mult, ALU.subtract),
                        [ps_tk, var_tk], [var_tk])
                rec.act(lambda e, o=var[:, :]: e.activation(o, o, AF.Sqrt, bias=gne[:, 0:1]), [var_tk, cst_tk], [var_tk])
                rec.dve(lambda e, o=var[:, :]: e.reciprocal(o, o), [var_tk], [var_tk])
                rec.dve(lambda e, o=mean[:, :], a=y32[t][:, :]: e.tensor_tensor(o, a, o, ALU.subtract), [y32_tk[t], mean_tk], [mean_tk])
                rec.dve(lambda e, o=mean[:, :], b=var[:, :]: e.tensor_tensor(o, o, b, ALU.mult), [mean_tk, var_tk], [mean_tk])
                rec.dve(lambda e, o=mean[:, :], t=t: e.tensor_scalar(o, o, vec[:, t, 5:6], vec[:, t, 6:7], ALU.mult, ALU.add),
                        [mean_tk, cst_tk], [mean_tk])
                rec.dve(lambda e, o=mean[:, :], b=bonus[t][:, :]: e.tensor_tensor(o, o, b, ALU.add), [mean_tk, bonus_tk[t]], [mean_tk])
                rec.dve(lambda e, o=yo[t][:, :], a=mean[:, :], b=inp[:, 5, t, :]: e.tensor_tensor(o, a, b, ALU.mult),
                        [mean_tk, inp_tk[5]], [yo_tk[t]])
            for m4 in range(4):
                os_, os_tk = ost[osti % 2], ost_tk[osti % 2]
                osti += 1
                for mm in range(4):
                    m = m4 * 4 + mm
                    ps, ps_tk = pp.get()
                    for k in range(4):
                        rec.pe(lambda e, o=ps[:, 0:BLK], l=wo[:, k, m * 128:(m + 1) * 128], r=yo[k][:, :], k=k:
                               e.matmul(o, l, r, start=(k == 0), stop=(k == 3)), [wo_tk, yo_tk[k]], [ps_tk])
                    if mm % 2 == 0:
                        rec.act(lambda e, o=os_[:, mm, :], i=ps[:, 0:BLK]: e.activation(o, i, AF.Copy), [ps_tk], [os_tk])
                    else:
                        rec.dve(lambda e, o=os_[:, mm, :], i=ps[:, 0:BLK]: e.tensor_copy(o, i), [ps_tk], [os_tk])
                otk = Tk()
                out_tks.append(otk)
                rec.dma("sp", lambda e, o=out_v[:, m4 * 4:m4 * 4 + 4, sl], i=os_[:, :, :]: e.dma_start(out=o, in_=i),
                        [os_tk], [otk])
        rec.add("sp", None, out_tks, [])
        cx.finish()
    return nc


def build_norm_kernel():
    from contextlib import ExitStack
    nc = bass.Bass("TRN2", target_bir_lowering=False)
    x_d = nc.dram_tensor("hT", [D, TOK], F32, kind="ExternalInput").ap()
    g_d = nc.dram_tensor("g_out", [128, 16], F32, kind="ExternalInput").ap()
    hn_d = nc.dram_tensor("hnT", [D, TOK], BF16, kind="ExternalOutput").ap()
    with ExitStack() as stack:
        cx = Ctx(nc, stack)
        rec = cx.rec
        dr = Tk()
        ones, ones_tk = setup_consts(cx)
        pp = PsumPool(cx, 4)
        KT = 16
        h_sb = cx.sb([128, KT, TOK], F32, "h")
        h_tk = [Tk() for _ in range(KT)]
        hn_sb = cx.sb([128, KT, TOK], BF16, "hn")
        hn_tk = [Tk() for _ in range(KT)]
        g_sb = cx.sb([128, 16], F32, "g")
        g_tk = Tk()
        scr = cx.sb([128, 2, TOK], BF16, "scr")
        scr_tk = [Tk(), Tk()]
        rstd = cx.sb([128, TOK], F32, "rstd")
        rstd_tk = Tk()
        rec.dma("sp", lambda e: e.dma_start(out=g_sb[:, :], in_=g_d), [dr], [g_tk])
        x_v = x_d.rearrange("(kc p) t -> p kc t", p=128)
        hn_v = hn_d.rearrange("(kc p) t -> p kc t", p=128)
        for k in range(KT):
            rec.dma("sp", lambda e, k=k: e.dma_start(out=h_sb[:, k, :], in_=x_v[:, k, :]), [dr], [h_tk[k]])
        out_tks = []

        def after(k):
            otk = Tk()
            out_tks.append(otk)
            rec.dma("sp", lambda e, k=k: e.dma_start(out=hn_v[:, k, :], in_=hn_sb[:, k, :]), [hn_tk[k]], [otk])

        rmsnorm_fm(cx, pp, [h_sb[:, k, :] for k in range(KT)], h_tk, g_sb, g_tk, 0, TOK,
                   [hn_sb[:, k, :] for k in range(KT)], hn_tk, ones, ones_tk,
                   [scr[:, 0, :], scr[:, 1, :]], scr_tk, rstd[:, :], rstd_tk, after=after)
        rec.add("sp", None, out_tks, [])
        cx.finish()
    return nc


_NC_CACHE = {}


def _get_nc(name):
    if name not in _NC_CACHE:
        _NC_CACHE[name] = {
            "norm": build_norm_kernel,
            "mlp": lambda: build_mlp_kernel(False),
            "mlp_final": lambda: build_mlp_kernel(True),
            "rglru": build_rglru_kernel,
            "rwkv1": build_rwkv1_kernel,
            "rwkv2": build_rwkv2_kernel,
        }[name]()
    return _NC_CACHE[name]


def _run(name, in_maps):
    nc = _get_nc(name)
    res = run_bass_kernel_spmd(nc, in_maps, core_ids=list(range(NCORE)))
    return res.results


def glay(g):
    return np.ascontiguousarray(np.asarray(g, np.float32).reshape(16, 128).T)


def kernel(**inp):
    f = lambda k: np.asarray(inp[k], dtype=np.float32)
    x = f("x")
    norm_mix_g, norm_mlp_g, final_g = f("norm_mix_g"), f("norm_mlp_g"), f("final_norm_g")
    w_up, w_down = f("w_mlp_up"), f("w_mlp_down")
    cores = [(b, c) for b in range(B) for c in range(4)]
    hT = [np.ascontiguousarray(x[b, c * TOK:(c + 1) * TOK, :].T) for (b, c) in cores]
    res = _run("norm", [{"hT": hT[i], "g_out": glay(norm_mix_g[0])} for i in range(NCORE)])
    hn = [r["hnT"] for r in res]
    for i in range(DEPTH):
        j = i // 2
        hn_full = [np.ascontiguousarray(np.concatenate([hn[b * 4 + c] for c in range(4)], axis=1)) for b in range(B)]
        if i % 2 == 0:
            ims = [rglru_inputs(g, hn_full[b], f("rg_w_in")[j], f("rg_conv_w")[j], f("rg_conv_b")[j],
                                f("rg_gx_w")[j], f("rg_gx_b")[j], f("rg_ga_w")[j], f("rg_ga_b")[j],
                                f("rg_lambda")[j], f("rg_w_out")[j]) for (b, g) in cores]
            res = _run("rglru", ims)
        else:
            vecs = [rwkv_vec(g, f("rw_w0")[j], f("rw_a0")[j], f("rw_k_k")[j], f("rw_k_a")[j], f("rw_r_k")[j],
                             f("rw_ln_g")[j], f("rw_ln_b")[j]) for (b, g) in cores]
            ims = [rwkv1_inputs(g, hn_full[b], f("rw_mu")[j], f("rw_w_rkv")[j], f("rw_w1")[j], f("rw_a1")[j],
                                f("rw_g1")[j], f("rw_w2")[j], f("rw_a2")[j], f("rw_g2")[j], vecs[ci])
                   for ci, (b, g) in enumerate(cores)]
            res = _run("rwkv1", ims)
            cst = rwkv2_consts()
            ims = []
            for ci, (b, g) in enumerate(cores):
                d = {"proj": res[ci]["proj"], "vec": vecs[ci],
                     "w_out": np.ascontiguousarray(f("rw_w_out")[j][CH * g:CH * g + CH, :])}
                d.update(cst)
                ims.append(d)
            res = _run("rwkv2", ims)
        parts = [np.ascontiguousarray(np.stack([res[b * 4 + g]["partT"][:, c * TOK:(c + 1) * TOK] for g in range(4)]))
                 for (b, c) in cores]
        last = (i == DEPTH - 1)
        g_out = final_g if last else norm_mix_g[i + 1]
        ims = [{"hT": hT[ci], "parts": parts[ci], "g_mlp": glay(norm_mlp_g[i]), "g_out": glay(g_out),
                "w_up": w_up[i], "w_down": w_down[i]} for ci in range(NCORE)]
        res = _run("mlp_final" if last else "mlp", ims)
        hT = [r["h2T"] for r in res]
        hn = [r["hnT"] for r in res]
    out = np.zeros((B, S, D), np.float32)
    for ci, (b, c) in enumerate(cores):
        out[b, c * TOK:(c + 1) * TOK, :] = np.asarray(hn[ci], np.float32).T
    return out
```
